# Optimizing a Trainium2 kernel written in Bass

```python
import math
import jax, jax.numpy as jnp
from jax import lax
import numpy as np

D_MODEL = 2048
BATCH = 2
SEQ = 16384
DEPTH = 4

N_MIXERS = 3
EPS = 1e-6

A_GROUPS = ((128, 1), (512, 4), (2048, 16))
A_N_GROUPS = 3
A_HEADS_PER_GROUP = 8
A_HEAD_DIM = 128
A_WIDTH = A_HEADS_PER_GROUP * A_HEAD_DIM
A_QKV_COLS = 3 * A_N_GROUPS * A_WIDTH
A_IN_COLS = A_QKV_COLS + A_WIDTH
A_ROT_DIM = A_HEAD_DIM // 4
A_ROPE_THETA = 500000.0
A_BLOCK = 128

B_WIDTH = D_MODEL
B_CONV_WIDTH = 31
B_IN_COLS = 3 * B_WIDTH

C_QK_DIM = 256
C_HEADS = D_MODEL // C_QK_DIM
C_V_DIM = 2 * C_QK_DIM
C_QK_WIDTH = C_HEADS * C_QK_DIM
C_V_WIDTH = C_HEADS * C_V_DIM
C_IN_COLS = 2 * C_QK_WIDTH + 2 * C_V_WIDTH
C_CHUNK = 128
C_ROPE_THETA = 10000.0

N_A = (DEPTH + 2) // 3
N_B = (DEPTH + 1) // 3
N_C = DEPTH // 3

kernel_name = "hybrid_dilated_conv_retention_trunk"


def rmsnorm(x, w):
    x32 = x.astype(jnp.float32)
    y = x32 * lax.rsqrt(jnp.mean(x32 * x32, axis=-1, keepdims=True) + EPS)
    return (y * w.astype(jnp.float32)).astype(x.dtype)


def rotary(t, positions, rot_dim, theta):
    inv_freq = theta ** (-jnp.arange(0, rot_dim, 2, dtype=jnp.float32) / rot_dim)
    ang = positions.astype(jnp.float32)[..., None] * inv_freq
    cos = jnp.cos(ang)[:, :, None, :]
    sin = jnp.sin(ang)[:, :, None, :]
    t32 = t.astype(jnp.float32)
    half = rot_dim // 2
    t1 = t32[..., :half]
    t2 = t32[..., half:rot_dim]
    return jnp.concatenate([t1 * cos - t2 * sin, t2 * cos + t1 * sin, t32[..., rot_dim:]], axis=-1)


def dilated_window_attention(q, k, v, window, dilation):
    b, s, h, dh = q.shape
    L = s // dilation
    w = window // dilation
    qb_len = A_BLOCK
    nb = -(-L // qb_len)
    lp = nb * qb_len

    def to_sub(t):
        t = t.reshape(b, L, dilation, h, dh).transpose(0, 2, 1, 3, 4)
        return t.reshape(b * dilation, L, h, dh)

    qs, ks, vs = to_sub(q), to_sub(k), to_sub(v)
    n = b * dilation
    qb = jnp.pad(qs, ((0, 0), (0, lp - L), (0, 0), (0, 0))).reshape(n, nb, qb_len, h, dh)
    pad_k = ((0, 0), (qb_len, lp - L), (0, 0), (0, 0))
    kb = jnp.pad(ks, pad_k).reshape(n, nb + 1, qb_len, h, dh)
    vb = jnp.pad(vs, pad_k).reshape(n, nb + 1, qb_len, h, dh)
    kblk = jnp.concatenate([kb[:, :-1], kb[:, 1:]], axis=2)
    vblk = jnp.concatenate([vb[:, :-1], vb[:, 1:]], axis=2)

    scores = jnp.einsum('nbqhd,nbkhd->nbhqk', qb, kblk) * (dh ** -0.5)
    qi = jnp.arange(qb_len)[:, None]
    kj = jnp.arange(2 * qb_len)[None, :]
    dist = qb_len + qi - kj
    blk = jnp.arange(nb)[:, None, None]
    valid = (dist >= 0) & (dist <= w) & (blk * qb_len - qb_len + kj >= 0)
    scores = jnp.where(valid[None, :, None], scores, -jnp.inf)
    m = jnp.max(scores, axis=-1, keepdims=True)
    p = jnp.exp(scores - m)
    l = jnp.sum(p, axis=-1, keepdims=True)
    o = jnp.einsum('nbhqk,nbkhd->nbqhd', p / l, vblk)
    lse = (m + jnp.log(l))[..., 0].transpose(0, 1, 3, 2)

    def from_sub(t):
        t = t.reshape((n, lp) + t.shape[3:])[:, :L]
        t = t.reshape((b, dilation, L) + t.shape[2:])
        t = jnp.moveaxis(t, 1, 2)
        return t.reshape((b, s) + t.shape[3:])

    return from_sub(o), from_sub(lse)


def mixer_a(h, positions, w_in, w_out):
    b, s, _ = h.shape
    proj = h @ w_in
    qkv = proj[..., :A_QKV_COLS].reshape(b, s, A_N_GROUPS, 3, A_HEADS_PER_GROUP, A_HEAD_DIM)
    gate = proj[..., A_QKV_COLS:]
    outs, lses = [], []
    for g, (window, dilation) in enumerate(A_GROUPS):
        q = rotary(qkv[:, :, g, 0], positions, A_ROT_DIM, A_ROPE_THETA)
        k = rotary(qkv[:, :, g, 1], positions, A_ROT_DIM, A_ROPE_THETA)
        v = qkv[:, :, g, 2].astype(jnp.float32)
        o_g, lse_g = dilated_window_attention(q, k, v, window, dilation)
        outs.append(o_g)
        lses.append(lse_g)
    alpha = jax.nn.softmax(jnp.stack(lses, axis=0), axis=0)
    o = jnp.einsum('gbsh,gbshd->bshd', alpha, jnp.stack(outs, axis=0))
    y = o.reshape(b, s, A_WIDTH).astype(h.dtype) * jax.nn.silu(gate)
    return y @ w_out


def mixer_b(h, w_in, conv_w, conv_b, ln_w, ln_b, w_out):
    proj = h @ w_in
    val, glu_gate, gate = jnp.split(proj, 3, axis=-1)
    u = val * jax.nn.sigmoid(glu_gate)
    u_pad = jnp.pad(u, ((0, 0), (B_CONV_WIDTH - 1, 0), (0, 0)))
    c = lax.conv_general_dilated(u_pad, conv_w[:, None, :].astype(u.dtype), window_strides=(1,),
                                 padding='VALID', dimension_numbers=('NWC', 'WIO', 'NWC'),
                                 feature_group_count=B_WIDTH) + conv_b
    c32 = c.astype(jnp.float32)
    mu = jnp.mean(c32, axis=-1, keepdims=True)
    var = jnp.mean(jnp.square(c32 - mu), axis=-1, keepdims=True)
    cn = ((c32 - mu) * lax.rsqrt(var + EPS) * ln_w + ln_b).astype(h.dtype)
    y = jax.nn.silu(cn) * jax.nn.silu(gate)
    return y @ w_out


def mixer_c(h, positions, w_in, w_out):
    b, s, _ = h.shape
    proj = h @ w_in
    q = proj[..., :C_QK_WIDTH].reshape(b, s, C_HEADS, C_QK_DIM)
    k = proj[..., C_QK_WIDTH:2 * C_QK_WIDTH].reshape(b, s, C_HEADS, C_QK_DIM)
    v = proj[..., 2 * C_QK_WIDTH:2 * C_QK_WIDTH + C_V_WIDTH].reshape(b, s, C_HEADS, C_V_DIM)
    gate = proj[..., 2 * C_QK_WIDTH + C_V_WIDTH:]
    q = rotary(q, positions, C_QK_DIM, C_ROPE_THETA)
    k = rotary(k, positions, C_QK_DIM, C_ROPE_THETA) * (C_QK_DIM ** -0.5)
    v = v.astype(jnp.float32)

    gammas = 1.0 - jnp.exp(jnp.linspace(math.log(1.0 / 32), math.log(1.0 / 512), C_HEADS, dtype=jnp.float32))
    log_g = jnp.log(gammas)
    idx = jnp.arange(C_CHUNK, dtype=jnp.float32)
    diff = idx[:, None] - idx[None, :]
    decay_mask = jnp.where(diff >= 0, jnp.exp(jnp.maximum(diff, 0.0)[None] * log_g[:, None, None]), 0.0)
    q_decay = jnp.exp((idx[None] + 1.0) * log_g[:, None])
    k_decay = jnp.exp((C_CHUNK - 1.0 - idx[None]) * log_g[:, None])
    chunk_decay = jnp.exp(C_CHUNK * log_g)

    n = s // C_CHUNK

    def chunks(t):
        return t.reshape(b, n, C_CHUNK, C_HEADS, t.shape[-1]).transpose(1, 0, 3, 2, 4)

    def step(state, xs):
        qc, kc, vc = xs
        inner = jnp.einsum('bhid,bhjd->bhij', qc, kc) * decay_mask
        o = (jnp.einsum('bhij,bhjv->bhiv', inner, vc)
             + jnp.einsum('bhid,bhdv->bhiv', qc, state) * q_decay[:, :, None])
        state = (state * chunk_decay[:, None, None]
                 + jnp.einsum('bhjd,bhjv->bhdv', kc * k_decay[:, :, None], vc))
        return state, o

    state0 = jnp.zeros((b, C_HEADS, C_QK_DIM, C_V_DIM), jnp.float32)
    _, o = lax.scan(step, state0, (chunks(q), chunks(k), chunks(v)))
    o = o.transpose(1, 0, 3, 2, 4).reshape(b, s, C_HEADS, C_V_DIM)
    mu = jnp.mean(o, axis=-1, keepdims=True)
    var = jnp.mean(jnp.square(o - mu), axis=-1, keepdims=True)
    o = (o - mu) * lax.rsqrt(var + EPS)
    y = o.reshape(b, s, C_V_WIDTH).astype(h.dtype) * jax.nn.silu(gate)
    return y @ w_out


def setup_inputs(seed: int = 0) -> dict:
    key = jax.random.key(seed)
    ks = jax.random.split(key, 16)
    f32 = jnp.float32

    def dense(k, shape, fan_in):
        return jax.random.normal(k, shape, f32) * (fan_in ** -0.5)

    x = jax.random.normal(ks[0], (BATCH, SEQ, D_MODEL), f32)
    offset = jax.random.randint(ks[1], (BATCH, 1), 0, 4096, dtype=jnp.int32)
    positions = (jnp.arange(SEQ, dtype=jnp.int32)[None, :] + offset).astype(jnp.int32)
    norm_w = 1.0 + 0.02 * jax.random.normal(ks[2], (DEPTH, D_MODEL), f32)
    final_norm_w = 1.0 + 0.02 * jax.random.normal(ks[3], (D_MODEL,), f32)
    a_w_in = dense(ks[4], (N_A, D_MODEL, A_IN_COLS), D_MODEL)
    a_w_out = dense(ks[5], (N_A, A_WIDTH, D_MODEL), A_WIDTH)
    b_w_in = dense(ks[6], (N_B, D_MODEL, B_IN_COLS), D_MODEL)
    b_conv_w = dense(ks[7], (N_B, B_CONV_WIDTH, B_WIDTH), B_CONV_WIDTH)
    b_conv_b = 0.01 * jax.random.normal(ks[8], (N_B, B_WIDTH), f32)
    b_ln_w = 1.0 + 0.02 * jax.random.normal(ks[9], (N_B, B_WIDTH), f32)
    b_ln_b = 0.01 * jax.random.normal(ks[10], (N_B, B_WIDTH), f32)
    b_w_out = dense(ks[11], (N_B, B_WIDTH, D_MODEL), B_WIDTH)
    c_w_in = dense(ks[12], (N_C, D_MODEL, C_IN_COLS), D_MODEL)
    c_w_out = dense(ks[13], (N_C, C_V_WIDTH, D_MODEL), C_V_WIDTH)
    return {"x": x, "positions": positions, "norm_w": norm_w, "final_norm_w": final_norm_w,
            "a_w_in": a_w_in, "a_w_out": a_w_out,
            "b_w_in": b_w_in, "b_conv_w": b_conv_w, "b_conv_b": b_conv_b,
            "b_ln_w": b_ln_w, "b_ln_b": b_ln_b, "b_w_out": b_w_out,
            "c_w_in": c_w_in, "c_w_out": c_w_out}


def reference(x, positions, norm_w, final_norm_w, a_w_in, a_w_out, b_w_in, b_conv_w, b_conv_b,
              b_ln_w, b_ln_b, b_w_out, c_w_in, c_w_out):
    for i in range(DEPTH):
        kind = i % N_MIXERS
        j = i // N_MIXERS
        h = rmsnorm(x, norm_w[i])
        if kind == 0:
            y = mixer_a(h, positions, a_w_in[j], a_w_out[j])
        elif kind == 1:
            y = mixer_b(h, b_w_in[j], b_conv_w[j], b_conv_b[j], b_ln_w[j], b_ln_b[j], b_w_out[j])
        else:
            y = mixer_c(h, positions, c_w_in[j], c_w_out[j])
        x = x + y.astype(x.dtype)
    return rmsnorm(x, final_norm_w)
```

```python
import math
import os
import numpy as np
import ml_dtypes
from contextlib import ExitStack
import concourse.bass as bass
import concourse.mybir as mybir
from concourse.bass_utils import run_bass_kernel_spmd

F32 = mybir.dt.float32
BF16 = mybir.dt.bfloat16
I32 = mybir.dt.int32
AF = mybir.ActivationFunctionType
ALU = mybir.AluOpType

PE, ACT, DVE, POOL, SP = "pe", "act", "dve", "pool", "sp"
ENGS = (PE, ACT, DVE, POOL, SP)
N_DMA_SEMS = 24
D = 2048
KC = 16
NCORES = 8
PI = math.pi
TWO_PI = 2.0 * math.pi
CW1 = 6.28125
CW2 = TWO_PI - CW1


class Sched:
    def __init__(self, nc):
        self.nc = nc
        self.ges = ExitStack()
        self.sems = None
        self.prev_sems = None
        self.dsems = [self.ges.enter_context(nc.semaphore(f"s_d{k}")) for k in range(N_DMA_SEMS)]
        self.cnt = {e: 0 for e in ENGS}
        self.dcnt = [0] * N_DMA_SEMS
        self.n_alloc = 0
        self.n_stage = 0
        self.total_ops = 0
        self.stage_begin()

    def stage_begin(self):
        self.ops = []
        self.eng_count = {e: 0 for e in ENGS}
        self.last_writer = {}
        self.readers = {}
        self.dma_rr = 0
        self.dma_last = [None] * N_DMA_SEMS
        self.es = ExitStack()

    def finish(self):
        self.ges.close()

    def simulate(self):
        val = {}
        pc = {e: 0 for e in ENGS}
        st = self.sim_streams
        progress = True
        while progress:
            progress = False
            for e in ENGS:
                while pc[e] < len(st[e]):
                    wl, inc = st[e][pc[e]]
                    if any(val.get(k, 0) < v for k, v in wl):
                        break
                    if inc is not None:
                        val[inc[0]] = val.get(inc[0], 0) + inc[1]
                    pc[e] += 1
                    progress = True
        stuck = {e: (pc[e], len(st[e])) for e in ENGS if pc[e] < len(st[e])}
        detail = {e: [(k, v, val.get(k, 0)) for k, v in st[e][pc[e]][0] if val.get(k, 0) < v] for e in stuck}
        return stuck, detail, val

    def sb(self, shape, dt, name=None):
        self.n_alloc += 1
        return self.es.enter_context(self.nc.sbuf_tensor((name or "sb") + f"_{self.n_alloc}", list(shape), dt))

    def ps(self, shape, dt, name=None):
        self.n_alloc += 1
        return self.es.enter_context(self.nc.psum_tensor((name or "ps") + f"_{self.n_alloc}", list(shape), dt))

    def _add(self, eng, fn, reads, writes, dma):
        oid = len(self.ops)
        deps = set()
        for r in reads:
            w = self.last_writer.get(r)
            if w is not None:
                deps.add(w)
        for r in writes:
            w = self.last_writer.get(r)
            if w is not None:
                deps.add(w)
            rd = self.readers.get(r)
            if rd:
                deps.update(rd[0].values())
                deps.update(rd[1])
        op = dict(id=oid, eng=eng, fn=fn, deps=deps, dma=dma, idx=self.eng_count[eng], sig=False)
        self.eng_count[eng] += 1
        if dma:
            k = self.dma_rr
            self.dma_rr = (k + 1) % N_DMA_SEMS
            op["dsem"] = k
            op["dprev"] = self.dma_last[k]
            self.dma_last[k] = oid
        self.ops.append(op)
        for r in reads:
            rd = self.readers.setdefault(r, ({}, []))
            if dma:
                rd[1].append(oid)
            else:
                rd[0][eng] = oid
        for r in writes:
            self.last_writer[r] = oid
            self.readers[r] = ({}, [])
        return oid

    def op(self, eng, fn, reads=(), writes=()):
        return self._add(eng, fn, tuple(reads), tuple(writes), False)

    def dma(self, eng, out, in_, reads=(), writes=()):
        return self._add(eng, lambda e: e.dma_start(out=out, in_=in_), tuple(reads), tuple(writes), True)

    def mm(self, out, lhsT, rhs, start, stop, r, w):
        self.op(PE, lambda e: e.matmul(out=out, lhsT=lhsT, rhs=rhs, start=start, stop=stop), r, w)

    def tr(self, out, in_, ident, r, w):
        self.op(PE, lambda e: e.transpose(out=out, in_=in_, identity=ident), r, w)

    def act(self, out, in_, func, r, w, **kw):
        self.op(ACT, lambda e: e.activation(out=out, in_=in_, func=func, **kw), r, w)

    def tt(self, eng, out, in0, in1, op, r, w):
        self.op(eng, lambda e: e.tensor_tensor(out=out, in0=in0, in1=in1, op=op), r, w)

    def ts(self, eng, out, in0, s1, s2, op0, op1, r, w):
        if op1 is None:
            self.op(eng, lambda e: e.tensor_scalar(out=out, in0=in0, scalar1=s1, scalar2=None, op0=op0), r, w)
        else:
            self.op(eng, lambda e: e.tensor_scalar(out=out, in0=in0, scalar1=s1, scalar2=s2, op0=op0, op1=op1), r, w)

    def stt(self, out, in0, scalar, in1, op0, op1, r, w):
        self.op(DVE, lambda e: e.scalar_tensor_tensor(out=out, in0=in0, scalar=scalar, in1=in1, op0=op0, op1=op1), r, w)

    def cp(self, eng, out, in_, r, w):
        if eng == ACT:
            self.op(ACT, lambda e: e.activation(out=out, in_=in_, func=AF.Copy), r, w)
        else:
            self.op(eng, lambda e: e.tensor_copy(out=out, in_=in_), r, w)

    def emit(self):
        nc = self.nc
        ops = self.ops
        waited = {e: {} for e in ENGS}
        waits = [None] * len(ops)
        for op in ops:
            e = op["eng"]
            need = []
            if op["dma"] and op["dprev"] is not None:
                need.append(op["dprev"])
            need.extend(sorted(op["deps"]))
            wl = []
            for d in need:
                p = ops[d]
                if p["dma"]:
                    key = ("d", p["dsem"])
                    if waited[e].get(key, -1) >= p["id"]:
                        continue
                    waited[e][key] = p["id"]
                else:
                    pe_ = p["eng"]
                    if pe_ == e and (pe_ == PE or p["idx"] < op["idx"] - 1):
                        continue
                    key = ("e", pe_)
                    if waited[e].get(key, -1) >= p["idx"]:
                        continue
                    waited[e][key] = p["idx"]
                wl.append(d)
                p["sig"] = True
            waits[op["id"]] = wl
        by_eng = {e: [op for op in ops if op["eng"] == e] for e in ENGS}
        for e in ENGS:
            if by_eng[e] and not by_eng[e][-1]["dma"]:
                by_eng[e][-1]["sig"] = True
        start_cnt = dict(self.cnt)
        start_dcnt = list(self.dcnt)
        self.cnt = {e: 0 for e in ENGS}
        cnt, dcnt = self.cnt, self.dcnt
        self.prev_sems = self.sems
        self.sems = {e: self.ges.enter_context(nc.semaphore(f"s_{e}_{self.n_stage}")) for e in ENGS}
        psems = self.prev_sems
        stg = self.n_stage
        for op in ops:
            if op["dma"]:
                dcnt[op["dsem"]] += 16
                op["val"] = dcnt[op["dsem"]]
            elif op["sig"]:
                cnt[op["eng"]] += 1
                op["val"] = cnt[op["eng"]]
        sems, dsems = self.sems, self.dsems
        first_stage = self.n_stage == 0
        self.n_stage += 1
        self.total_ops += len(ops)

        if not hasattr(self, "sim_streams"):
            self.sim_streams = {e: [] for e in ENGS}
        for en in ENGS:
            st = self.sim_streams[en]
            if not first_stage:
                pw = [(("e", e2, stg - 1), start_cnt[e2]) for e2 in ENGS if e2 != en and start_cnt[e2]]
                pw += [(("d", k), start_dcnt[k]) for k in range(N_DMA_SEMS) if start_dcnt[k]]
                st.append((pw, None))
            for op in by_eng[en]:
                wl = []
                for d in waits[op["id"]]:
                    p = ops[d]
                    wl.append(((("d", p["dsem"]), p["val"])) if p["dma"] else ((("e", p["eng"], stg), p["val"])))
                inc = (("d", op["dsem"]), 16) if op["dma"] else ((("e", en, stg), 1) if op["sig"] else None)
                st.append((wl, inc))
            if en == SP:
                st.append(([(("d", k), dcnt[k]) for k in range(N_DMA_SEMS) if dcnt[k]], None))

        def run(en, eng):
            if not first_stage:
                for e2 in ENGS:
                    if e2 != en and start_cnt[e2]:
                        eng.wait_ge(psems[e2], start_cnt[e2])
                for k in range(N_DMA_SEMS):
                    if start_dcnt[k]:
                        eng.wait_ge(dsems[k], start_dcnt[k])
            for op in by_eng[en]:
                for d in waits[op["id"]]:
                    p = ops[d]
                    if p["dma"]:
                        eng.wait_ge(dsems[p["dsem"]], p["val"])
                    else:
                        eng.wait_ge(sems[p["eng"]], p["val"])
                inst = op["fn"](eng)
                if op["dma"]:
                    inst.then_inc(dsems[op["dsem"]], 16)
                elif op["sig"]:
                    inst.then_inc(sems[en], 1)
            if en == SP:
                for k in range(N_DMA_SEMS):
                    if dcnt[k]:
                        eng.wait_ge(dsems[k], dcnt[k])

        with nc.Block() as block:
            @block.tensor
            def _(e):
                run(PE, e)

            @block.scalar
            def _(e):
                run(ACT, e)

            @block.vector
            def _(e):
                run(DVE, e)

            @block.gpsimd
            def _(e):
                run(POOL, e)

            @block.sync
            def _(e):
                run(SP, e)
        self.es.close()
        self.stage_begin()


def load_consts(S, nc, ident_ap):
    idf = S.sb([128, 128], F32, "idf")
    idb = S.sb([128, 128], BF16, "idb")
    S.dma(SP, idf[:], ident_ap, writes=["idf"])
    S.cp(DVE, idb[:], idf[:], ["idf"], ["idb"])
    return idb


def rot_tables(S, nrows, ntok, pos_row_ap, invf, sscale, Ct, St, tmp, tagw, rname):
    R = slice(0, nrows)
    posi, ang, kf, ki, r1, m1 = tmp["posi"], tmp["ang"], tmp["kf"], tmp["ki"], tmp["r1"], tmp["m1"]
    n = ntok
    S.dma(SP, posi[R, 0:n], pos_row_ap.partition_broadcast(nrows), writes=[rname + "posi"])
    S.cp(DVE, ang[R, 0:n], posi[R, 0:n], [rname + "posi"], [rname + "ang"])
    S.ts(DVE, ang[R, 0:n], ang[R, 0:n], invf, None, ALU.mult, None, [rname + "ang", "consts"], [rname + "ang"])
    S.ts(DVE, kf[R, 0:n], ang[R, 0:n], 1.0 / TWO_PI, None, ALU.mult, None, [rname + "ang"], [rname + "kf"])
    S.cp(DVE, ki[R, 0:n], kf[R, 0:n], [rname + "kf"], [rname + "ki"])
    S.cp(DVE, kf[R, 0:n], ki[R, 0:n], [rname + "ki"], [rname + "kf"])
    S.stt(r1[R, 0:n], kf[R, 0:n], -CW1, ang[R, 0:n], ALU.mult, ALU.add, [rname + "kf", rname + "ang"], [rname + "r1"])
    S.stt(r1[R, 0:n], kf[R, 0:n], -CW2, r1[R, 0:n], ALU.mult, ALU.add, [rname + "kf", rname + "r1"], [rname + "r1"])

    def wrap(src_tag):
        S.ts(DVE, m1[R, 0:n], r1[R, 0:n], PI, -TWO_PI, ALU.is_gt, ALU.mult, [src_tag], [rname + "m1"])
        S.tt(DVE, r1[R, 0:n], r1[R, 0:n], m1[R, 0:n], ALU.add, [src_tag, rname + "m1"], [src_tag])
        S.ts(DVE, m1[R, 0:n], r1[R, 0:n], -PI, TWO_PI, ALU.is_lt, ALU.mult, [src_tag], [rname + "m1"])
        S.tt(DVE, r1[R, 0:n], r1[R, 0:n], m1[R, 0:n], ALU.add, [src_tag, rname + "m1"], [src_tag])
        S.ts(DVE, r1[R, 0:n], r1[R, 0:n], -PI, PI, ALU.max, ALU.min, [src_tag], [src_tag])

    wrap(rname + "r1")
    S.act(St, r1[R, 0:n], AF.Sin, [rname + "r1", "consts"], tagw[1], scale=sscale)
    S.ts(DVE, r1[R, 0:n], r1[R, 0:n], PI / 2.0, None, ALU.add, None, [rname + "r1"], [rname + "r1"])
    wrap(rname + "r1")
    S.act(Ct, r1[R, 0:n], AF.Sin, [rname + "r1"], tagw[0])


def alloc_rot_tmp(S, nrows_alloc, n):
    return dict(posi=S.sb([nrows_alloc, n], I32, "rt_posi"), ang=S.sb([nrows_alloc, n], F32, "rt_ang"),
                kf=S.sb([nrows_alloc, n], F32, "rt_kf"), ki=S.sb([nrows_alloc, n], I32, "rt_ki"),
                r1=S.sb([nrows_alloc, n], F32, "rt_r1"), m1=S.sb([nrows_alloc, n], F32, "rt_m1"))


def emit_opn(S, nc, NT, W, final, x, nw_row, ident, yT=None, wout=None, out=None, hT=None, xo=None, eps=1e-6):
    KW = W // 128
    G = 1 if W >= 4096 else 4
    assert NT % G == 0
    ntok = NT * 128
    if W:
        yT_v = yT.rearrange("(kw p) t -> p kw t", p=128)
    if not final:
        hT_v = hT.rearrange("(kc p) t -> p kc t", p=128)
    idb = load_consts(S, nc, ident)
    wtab = S.sb([128, D], F32, "wtab")
    S.dma(SP, wtab[:], nw_row.partition_broadcast(128), writes=["wtab"])
    epsT = S.sb([128, 1], F32, "epsT")
    S.op(DVE, lambda e: e.memset(epsT[:], eps), writes=["eps"])
    if W:
        wsb = S.sb([128, KW, D], BF16, "wsb")
        for kw in range(KW):
            S.dma(POOL, wsb[:, kw, :], wout[kw * 128:(kw + 1) * 128, :], writes=[f"w{kw}"])
        ytl = [S.sb([128, KW, G * 128], BF16, f"ytl{i}") for i in range(2)]
        po = [S.ps([128, 512], F32, f"po{i}") for i in range(4)]
    NB = 2
    xt = [S.sb([128, D], F32, f"xt{i}") for i in range(NB)]
    junk = S.sb([128, D], BF16, "junk")
    ss = [S.sb([128, 1], F32, f"ss{i}") for i in range(NB)]
    rs = [S.sb([128, 1], F32, f"rs{i}") for i in range(NB)]
    if final:
        ho = [S.sb([128, D], F32, f"ho{i}") for i in range(NB)]
    else:
        hb = [S.sb([128, D], BF16, f"hb{i}") for i in range(NB)]
        hTs = [S.sb([128, KC, G * 128], BF16, f"hTs{i}") for i in range(2)]
        pst = [S.ps([128, 8, 128], BF16, f"pst{i}") for i in range(2)]
    for i in range(NT):
        b = i % NB
        g, gi = divmod(i, G)
        gb = g % 2
        tok = slice(i * 128, (i + 1) * 128)
        S.dma(SP, xt[b][:], x[tok, :], writes=[f"xt{b}"])
        if W:
            if gi == 0:
                S.dma(SP, ytl[gb][:], yT_v[:, :, g * G * 128:(g + 1) * G * 128], writes=[f"ytl{gb}"])
            for nb in range(4):
                for kw in range(KW):
                    S.mm(po[nb][:], ytl[gb][:, kw, gi * 128:(gi + 1) * 128], wsb[:, kw, nb * 512:(nb + 1) * 512],
                         kw == 0, kw == KW - 1, [f"ytl{gb}", f"w{kw}"], [f"po{nb}"])
                S.tt(DVE, xt[b][:, nb * 512:(nb + 1) * 512], po[nb][:], xt[b][:, nb * 512:(nb + 1) * 512], ALU.add,
                     [f"po{nb}", f"xt{b}"], [f"xt{b}"])
            if not final:
                S.dma(POOL, xo[tok, :], xt[b][:], reads=[f"xt{b}"], writes=["xo"])
        S.act(junk[:], xt[b][:], AF.Square, [f"xt{b}"], ["junk", f"ss{b}"], accum_out=ss[b][:])
        S.act(ss[b][:], ss[b][:], AF.Sqrt, [f"ss{b}", "eps"], [f"ss{b}"], bias=epsT[:], scale=1.0 / D)
        S.op(DVE, lambda e, b=b: e.reciprocal(out=rs[b][:], in_=ss[b][:]), [f"ss{b}"], [f"rs{b}"])
        if final:
            S.stt(ho[b][:], xt[b][:], rs[b][:], wtab[:], ALU.mult, ALU.mult, [f"xt{b}", f"rs{b}", "wtab"], [f"ho{b}"])
            S.dma(POOL, out[tok, :], ho[b][:], reads=[f"ho{b}"], writes=["out"])
            continue
        S.stt(hb[b][:], xt[b][:], rs[b][:], wtab[:], ALU.mult, ALU.mult, [f"xt{b}", f"rs{b}", "wtab"], [f"hb{b}"])
        for half in range(2):
            for j in range(8):
                kc = half * 8 + j
                S.tr(pst[half][:, j, :], hb[b][:, kc * 128:(kc + 1) * 128], idb[:], [f"hb{b}", "idb"], [f"pst{half}"])
            S.cp(ACT if half == 0 else DVE, hTs[gb][:, half * 8:(half + 1) * 8, gi * 128:(gi + 1) * 128], pst[half][:],
                 [f"pst{half}"], [f"hTs{gb}_{gi}_{half}"])
        if gi == G - 1:
            S.dma(POOL, hT_v[:, :, g * G * 128:(g + 1) * G * 128], hTs[gb][:],
                  reads=[f"hTs{gb}_{q}_{h}" for q in range(G) for h in range(2)], writes=["hT"])
    S.emit()


A_DILS = (1, 4, 16)
CH = 2048
SBK = 512


def emit_ma(S, nc, nseq, slen, hT, w_full, pos, ident, cst, msk, perm, yT, nheads=8):
    ntok = nseq * slen
    nch = slen // CH
    hT_v = hT.rearrange("(kc p) t -> p kc t", p=128)
    w_v = w_full.rearrange("(kc p) c -> p kc c", p=128)
    idb = load_consts(S, nc, ident)
    cs = S.sb([128, 8], F32, "cs")
    S.dma(SP, cs[:], cst, writes=["consts"])
    mskf = S.sb([128, 256], F32, "mskf")
    mskb = S.sb([128, 2, 128], BF16, "mskb")
    S.dma(SP, mskf[:], msk, writes=["mskf"])
    S.cp(DVE, mskb[:].rearrange("p a b -> p (a b)"), mskf[:], ["mskf"], ["mskb"])
    permf = S.sb([32, 32], F32, "permf")
    permb = S.sb([32, 32], BF16, "permb")
    S.dma(SP, permf[:], perm, writes=["permf"])
    S.cp(DVE, permb[:], permf[:], ["permf"], ["permb"])
    onesb = S.sb([128, 128], BF16, "onesb")
    S.op(DVE, lambda e: e.memset(onesb[:], 1.0), writes=["onesb"])
    wsb = S.sb([128, KC, 1280], BF16, "wsb")
    hTs = [S.sb([128, KC, SBK], BF16, f"hTs{i}") for i in range(2)]
    QT = [S.sb([128, CH], BF16, f"QT{g}") for g in range(3)]
    KT = [[S.sb([128, CH], BF16, f"KT{g}_{p}") for p in range(2)] for g in range(3)]
    VT = [S.sb([128, CH], BF16, f"VT{g}") for g in range(3)]
    Vt = [[S.sb([128, 16, 128], BF16, f"Vt{g}_{p}") for p in range(2)] for g in range(3)]
    sg = S.sb([128, CH], BF16, "sg")
    accN = S.sb([128, CH], F32, "accN")
    accL = S.sb([128, CH], F32, "accL")
    yo = S.sb([128, CH], BF16, "yo")
    Ct = [S.sb([32, SBK], F32, f"Ct{i}") for i in range(2)]
    St = [S.sb([32, SBK], F32, f"St{i}") for i in range(2)]
    rtmp = alloc_rot_tmp(S, 32, SBK)
    t1 = [S.sb([32, SBK], F32, f"t1_{i}") for i in range(2)]
    t2 = [S.sb([32, SBK], F32, f"t2_{i}") for i in range(2)]
    PT = [S.sb([128, 2, 128], BF16, f"PT{i}") for i in range(2)]
    PM = [S.sb([128, 2, 128], BF16, f"PM{i}") for i in range(2)]
    pacc = [S.ps([128, SBK], F32, f"pacc{i}") for i in range(2)]
    psw_b = S.ps([128, SBK], F32, "psw")
    psw = psw_b[0:32, :]
    pv_b = S.ps([128, 1024], BF16, "pv")
    pv = pv_b[:, 0:128]
    pS_b = [S.ps([128, 4, 128], F32, f"pS{i}") for i in range(2)]
    pS = [t[:, 0:2, :] for t in pS_b]
    pN_b = S.ps([128, SBK], F32, "pN")
    pN = pN_b[:, 0:128]
    pL_b = S.ps([128, SBK], F32, "pL")
    pL = pL_b[:, 0:128]
    scale = 128.0 ** -0.5
    sbc = 0
    rc = 0
    tc = 0
    for hd, sq in [(h_, s_) for h_ in range(nheads) for s_ in range(nseq)]:
        if sq == 0:
            seg = 0
            for g_ in range(3):
                for t_ in range(3):
                    c0 = g_ * 3072 + t_ * 1024 + hd * 128
                    S.dma(POOL, wsb[:, :, seg * 128:(seg + 1) * 128], w_v[:, :, c0:c0 + 128], writes=[f"ws{seg}"])
                    seg += 1
            c0 = 9216 + hd * 128
            S.dma(POOL, wsb[:, :, 9 * 128:10 * 128], w_v[:, :, c0:c0 + 128], writes=["ws9"])
        for ch in range(nch):
            par = ch % 2
            base = sq * slen + ch * CH
            for sb_ in range(CH // SBK):
                hb = sbc % 2
                sbc += 1
                cols = slice(sb_ * SBK, (sb_ + 1) * SBK)
                t0 = base + sb_ * SBK
                S.dma(SP, hTs[hb][:], hT_v[:, :, t0:t0 + SBK], writes=[f"h{hb}"])
                rot_tables(S, 32, SBK, pos[0, t0:t0 + SBK], cs[0:32, 0:1], cs[0:32, 1:2], Ct[hb][:], St[hb][:], rtmp,
                           ([f"Ct{hb}"], [f"St{hb}"]), "rt_")
                for cb in range(10):
                    pa = pacc[cb % 2]
                    for kc in range(KC):
                        S.mm(pa[:], wsb[:, kc, cb * 128:(cb + 1) * 128], hTs[hb][:, kc, :], kc == 0, kc == KC - 1,
                             [f"ws{cb}", f"h{hb}"], [f"pacc{cb % 2}"])
                    if cb == 9:
                        S.act(sg[:, cols], pa[:], AF.Silu, [f"pacc{cb % 2}"], [f"sg_{sb_}"])
                        continue
                    g, typ = divmod(cb, 3)
                    if typ == 2:
                        S.cp(DVE, VT[g][:, cols], pa[:], [f"pacc{cb % 2}"], [f"VT{g}_{sb_}"])
                        continue
                    dst = QT[g] if typ == 0 else KT[g][par]
                    dn = (f"QT{g}_{sb_}" if typ == 0 else f"KT{g}_{par}_{sb_}")
                    S.cp(ACT, dst[:, cols], pa[:], [f"pacc{cb % 2}"], [dn])
                    rb = rc % 2
                    rc += 1
                    S.mm(psw, permb[:], dst[0:32, cols], True, True, ["permb", dn], ["psw"])
                    S.tt(POOL, t1[rb][:], dst[0:32, cols], Ct[hb][:], ALU.mult, [dn, f"Ct{hb}"], [f"t1_{rb}"])
                    S.tt(DVE, t2[rb][:], psw, St[hb][:], ALU.mult, ["psw", f"St{hb}"], [f"t2_{rb}"])
                    S.tt(DVE, dst[0:32, cols], t1[rb][:], t2[rb][:], ALU.add, [f"t1_{rb}", f"t2_{rb}"], [dn])
            allsb = range(CH // SBK)
            for g, dil in enumerate(A_DILS):
                nblk = 16 // dil
                qres = [f"QT{g}_{s}" for s in allsb]
                kres = [f"KT{g}_{par}_{s}" for s in allsb]
                kres_o = [f"KT{g}_{1 - par}_{s}" for s in allsb]
                vres = [f"VT{g}_{s}" for s in allsb]
                for b in range(nblk):
                    for r in range(dil):
                        ti = b * dil + r
                        cols = slice(b * 128 * dil + r, (b + 1) * 128 * dil, dil)
                        S.tr(pv, VT[g][:, cols], idb[:], vres + ["idb"], ["pv"])
                        S.cp(ACT if tc % 2 == 0 else DVE, Vt[g][par][:, ti, :], pv, ["pv"], [f"Vt{g}_{par}_{ti}"])
                        if b >= 1:
                            has_prev = True
                            pcols = slice((b - 1) * 128 * dil + r, b * 128 * dil, dil)
                            kprev, kpres = KT[g][par], kres
                            vprev, vpres = Vt[g][par][:, (b - 1) * dil + r, :], f"Vt{g}_{par}_{(b - 1) * dil + r}"
                        elif ch >= 1:
                            has_prev = True
                            pcols = slice((nblk - 1) * 128 * dil + r, nblk * 128 * dil, dil)
                            kprev, kpres = KT[g][1 - par], kres_o
                            vprev, vpres = Vt[g][1 - par][:, (nblk - 1) * dil + r, :], f"Vt{g}_{1 - par}_{(nblk - 1) * dil + r}"
                        else:
                            has_prev = False
                        sbi = tc % 2
                        tc += 1
                        S.mm(pS[sbi][:, 0, :], KT[g][par][:, cols], QT[g][:, cols], True, True, kres + qres, [f"pS{sbi}"])
                        if has_prev:
                            S.mm(pS[sbi][:, 1, :], kprev[:, pcols], QT[g][:, cols], True, True, kpres + qres, [f"pS{sbi}"])
                        nh = 2 if has_prev else 1
                        S.act(PT[sbi][:, 0:nh, :], pS[sbi][:, 0:nh, :], AF.Exp, [f"pS{sbi}"], [f"PT{sbi}"], scale=scale)
                        S.tt(POOL, PM[sbi][:, 0:nh, :], PT[sbi][:, 0:nh, :], mskb[:, 0:nh, :], ALU.mult,
                             [f"PT{sbi}", "mskb"], [f"PM{sbi}"])
                        S.mm(pN, Vt[g][par][:, ti, :], PM[sbi][:, 0, :], True, not has_prev,
                             [f"Vt{g}_{par}_{ti}", f"PM{sbi}"], ["pN"])
                        if has_prev:
                            S.mm(pN, vprev, PM[sbi][:, 1, :], False, True, [vpres, f"PM{sbi}"], ["pN"])
                        S.mm(pL, onesb[:], PM[sbi][:, 0, :], True, not has_prev, ["onesb", f"PM{sbi}"], ["pL"])
                        if has_prev:
                            S.mm(pL, onesb[:], PM[sbi][:, 1, :], False, True, ["onesb", f"PM{sbi}"], ["pL"])
                        if g == 0:
                            S.cp(ACT, accN[:, cols], pN, ["pN"], ["accN"])
                            S.cp(DVE, accL[:, cols], pL, ["pL"], ["accL"])
                        else:
                            S.tt(DVE, accN[:, cols], pN, accN[:, cols], ALU.add, ["pN", "accN"], ["accN"])
                            S.tt(DVE, accL[:, cols], pL, accL[:, cols], ALU.add, ["pL", "accL"], ["accL"])
            S.op(DVE, lambda e: e.reciprocal(out=accL[:], in_=accL[:]), ["accL"], ["accL"])
            S.tt(DVE, accN[:], accN[:], accL[:], ALU.mult, ["accN", "accL"], ["accN"])
            S.tt(POOL, yo[:], accN[:], sg[:], ALU.mult, ["accN"] + [f"sg_{s}" for s in allsb], ["yo"])
            S.dma(POOL, yT[hd * 128:(hd + 1) * 128, base:base + CH], yo[:], reads=["yo"], writes=["yT"])
    S.emit()


def ma_consts():
    inv = (500000.0 ** (-np.arange(0, 32, 2, dtype=np.float32) / np.float32(32))).astype(np.float32)
    cst = np.zeros((128, 8), np.float32)
    cst[0:16, 0] = inv
    cst[16:32, 0] = inv
    cst[0:16, 1] = -1.0
    cst[16:32, 1] = 1.0
    k = np.arange(128)[:, None]
    q = np.arange(128)[None, :]
    msk = np.concatenate([(q >= k), (k >= q)], axis=1).astype(np.float32)
    perm = np.zeros((32, 32), np.float32)
    for d in range(32):
        perm[(d + 16) % 32, d] = 1.0
    return cst, msk, perm


def ma_wcols(c):
    cols = []
    for g in range(3):
        for t in range(3):
            s = g * 3072 + t * 1024 + c * 128
            cols.extend(range(s, s + 128))
    cols.extend(range(9216 + c * 128, 9216 + (c + 1) * 128))
    return np.array(cols)


def emit_mc(S, nc, nseq, slen, hT, w_full, pos, ident, cst8, tabs8, yT, nheads=8, eps=1e-6):
    ntok = nseq * slen
    nsb = slen // SBK
    hT_v = hT.rearrange("(kc p) t -> p kc t", p=128)
    yT_v = yT.rearrange("(vb p) t -> p vb t", p=128)
    w_v = w_full.rearrange("(kc p) c -> p kc c", p=128)
    idb = load_consts(S, nc, ident)
    cs = S.sb([128, 8], F32, "cs")
    tb = S.sb([128, 128 + 2 * SBK], F32, "tb")
    maskT = tb[:, 0:128]
    QD = tb[:, 128:128 + SBK]
    KD = tb[:, 128 + SBK:128 + 2 * SBK]
    epsT = S.sb([128, 1], F32, "epsT")
    S.op(DVE, lambda e: e.memset(epsT[:], eps), writes=["eps"])
    wsb = S.sb([128, KC, 1536], BF16, "wsb")
    hTs = [S.sb([128, KC, SBK], BF16, f"hTs{i}") for i in range(2)]
    Ct = [S.sb([128, SBK], F32, f"Ct{i}") for i in range(2)]
    St = [S.sb([128, SBK], F32, f"St{i}") for i in range(2)]
    rtmp = alloc_rot_tmp(S, 128, SBK)
    ta = [S.sb([128, SBK], F32, f"ta{i}") for i in range(2)]
    tb2 = [S.sb([128, SBK], F32, f"tb2{i}") for i in range(2)]
    QT = [[S.sb([128, SBK], BF16, f"QT{i}_{b}") for b in range(2)] for i in range(2)]
    QpT = [[S.sb([128, SBK], BF16, f"QpT{i}_{b}") for b in range(2)] for i in range(2)]
    KT = [[S.sb([128, SBK], BF16, f"KT{i}_{b}") for b in range(2)] for i in range(2)]
    KpT = [[S.sb([128, SBK], BF16, f"KpT{i}_{b}") for b in range(2)] for i in range(2)]
    Vt = [S.sb([128, 4, 512], BF16, f"Vt{i}") for i in range(2)]
    SG = [S.sb([128, 4, 512], BF16, f"SG{i}") for i in range(2)]
    Kp = [S.sb([128, 4, 256], BF16, f"Kp{i}") for i in range(2)]
    st = [S.sb([128, 512], F32, f"st{b}") for b in range(2)]
    stb = [[S.sb([128, 512], BF16, f"stb{p}_{b}") for b in range(2)] for p in range(2)]
    PTm = [S.sb([128, 128], BF16, f"PTm{i}") for i in range(2)]
    bst = [S.sb([128, 6], F32, f"bst{i}") for i in range(2)]
    mv = [S.sb([128, 2], F32, f"mv{i}") for i in range(2)]
    rsd = [S.sb([128, 1], F32, f"rsd{i}") for i in range(2)]
    yn = [S.sb([128, 512], F32, f"yn{i}") for i in range(2)]
    yb = [S.sb([128, 512], BF16, f"yb{i}") for i in range(2)]
    yTs = [S.sb([128, 4, SBK], BF16, f"yTs{i}") for i in range(2)]
    pacc = [S.ps([128, 512], F32, f"pacc{i}") for i in range(2)]
    pI_b = S.ps([128, 512], F32, "pI")
    pI = pI_b[:, 0:128]
    pO = S.ps([128, 512], F32, "pO")
    pSt = [S.ps([128, 512], F32, f"pSt{b}") for b in range(2)]
    pT = S.ps([128, 8, 128], BF16, "pT")
    pK = S.ps([128, 8, 128], BF16, "pK")
    gc = 0
    sbc = 0
    pending = []
    for hd, sq in [(h_, s_) for h_ in range(nheads) for s_ in range(nseq)]:
        if sq == 0:
            for fn in pending:
                fn()
            pending = []
            S.dma(SP, cs[:], cst8[hd], writes=["consts"])
            S.dma(SP, tb[:], tabs8[hd], writes=["tb"])
            S.dma(POOL, wsb[:, :, 0:256], w_v[:, :, hd * 256:(hd + 1) * 256], writes=["wq"])
            S.dma(POOL, wsb[:, :, 256:512], w_v[:, :, 2048 + hd * 256:2048 + (hd + 1) * 256], writes=["wk"])
            S.dma(POOL, wsb[:, :, 512:1024], w_v[:, :, 4096 + hd * 512:4096 + (hd + 1) * 512], writes=["wv"])
            S.dma(POOL, wsb[:, :, 1024:1536], w_v[:, :, 8192 + hd * 512:8192 + (hd + 1) * 512], writes=["wg"])
        for sb_ in range(nsb):
            hb = sbc % 2
            sbc += 1
            t0 = sq * slen + sb_ * SBK
            S.dma(SP, hTs[hb][:], hT_v[:, :, t0:t0 + SBK], writes=[f"h{hb}"])
            rot_tables(S, 128, SBK, pos[0, t0:t0 + SBK], cs[:, 0:1], cs[:, 1:2], Ct[hb][:], St[hb][:], rtmp,
                       ([f"Ct{hb}"], [f"St{hb}"]), "rt_")
            for which in range(2):
                dstT = QT[hb] if which == 0 else KT[hb]
                dstP = QpT[hb] if which == 0 else KpT[hb]
                nm = "Q" if which == 0 else "K"
                dec = QD if which == 0 else KD
                for blk in range(2):
                    cb = which * 2 + blk
                    for kc in range(KC):
                        S.mm(pacc[blk][:], wsb[:, kc, cb * 128:(cb + 1) * 128], hTs[hb][:, kc, :], kc == 0, kc == KC - 1,
                             ["wq" if which == 0 else "wk", f"h{hb}"], [f"pacc{blk}"])
                S.tt(DVE, ta[0][:], pacc[0][:], Ct[hb][:], ALU.mult, ["pacc0", f"Ct{hb}"], ["ta0"])
                S.tt(DVE, tb2[0][:], pacc[1][:], St[hb][:], ALU.mult, ["pacc1", f"St{hb}"], ["tb0"])
                S.tt(DVE, ta[1][:], pacc[1][:], Ct[hb][:], ALU.mult, ["pacc1", f"Ct{hb}"], ["ta1"])
                S.tt(DVE, tb2[1][:], pacc[0][:], St[hb][:], ALU.mult, ["pacc0", f"St{hb}"], ["tb1"])
                S.tt(POOL, dstT[0][:], ta[0][:], tb2[0][:], ALU.subtract, ["ta0", "tb0"], [f"{nm}T{hb}_0"])
                S.tt(POOL, dstT[1][:], ta[1][:], tb2[1][:], ALU.add, ["ta1", "tb1"], [f"{nm}T{hb}_1"])
                for blk in range(2):
                    S.tt(POOL, dstP[blk][:], dstT[blk][:], dec, ALU.mult, [f"{nm}T{hb}_{blk}", "tb"], [f"{nm}pT{hb}_{blk}"])
            for c4 in range(4):
                tk = slice(c4 * 128, (c4 + 1) * 128)
                for kc in range(KC):
                    S.mm(pacc[0][:], hTs[hb][:, kc, tk], wsb[:, kc, 512:1024], kc == 0, kc == KC - 1,
                         ["wv", f"h{hb}"], ["pacc0"])
                S.cp(ACT, Vt[hb][:, c4, :], pacc[0][:], ["pacc0"], [f"Vt{hb}_{c4}"])
                for kc in range(KC):
                    S.mm(pacc[1][:], hTs[hb][:, kc, tk], wsb[:, kc, 1024:1536], kc == 0, kc == KC - 1,
                         ["wg", f"h{hb}"], ["pacc1"])
                S.act(SG[hb][:, c4, :], pacc[1][:], AF.Silu, ["pacc1"], [f"SG{hb}_{c4}"])
                for blk in range(2):
                    S.tr(pK[:, blk, :], KpT[hb][blk][:, tk], idb[:], [f"KpT{hb}_{blk}", "idb"], ["pK"])
                S.cp(DVE, Kp[hb][:, c4, :], pK[:, 0:2, :].rearrange("p a b -> p (a b)"), ["pK"], [f"Kp{hb}_{c4}"])
            for c4 in range(4):
                tk = slice(c4 * 128, (c4 + 1) * 128)
                first = (sb_ == 0 and c4 == 0)
                par = gc % 2
                cb2 = gc % 2
                gc += 1
                for blk in range(2):
                    S.mm(pSt[blk][:], Kp[hb][:, c4, blk * 128:(blk + 1) * 128], Vt[hb][:, c4, :], True, True,
                         [f"Kp{hb}_{c4}", f"Vt{hb}_{c4}"], [f"pSt{blk}"])
                S.mm(pI, KT[hb][0][:, tk], QT[hb][0][:, tk], True, False, [f"KT{hb}_0", f"QT{hb}_0"], ["pI"])
                S.mm(pI, KT[hb][1][:, tk], QT[hb][1][:, tk], False, True, [f"KT{hb}_1", f"QT{hb}_1"], ["pI"])
                S.tt(DVE, PTm[cb2][:], pI, maskT, ALU.mult, ["pI", "tb"], [f"PTm{cb2}"])
                S.mm(pO[:], PTm[cb2][:], Vt[hb][:, c4, :], True, first, [f"PTm{cb2}", f"Vt{hb}_{c4}"], ["pO"])
                if not first:
                    for blk in range(2):
                        S.mm(pO[:], QpT[hb][blk][:, tk], stb[1 - par][blk][:], False, blk == 1,
                             [f"QpT{hb}_{blk}", f"stb{1 - par}_{blk}"], ["pO"])
                for fn in pending:
                    fn()
                pending = []
                for blk in range(2):
                    if first:
                        S.cp(DVE, st[blk][:], pSt[blk][:], [f"pSt{blk}"], [f"st{blk}"])
                    else:
                        S.stt(st[blk][:], st[blk][:], cs[:, 2:3], pSt[blk][:], ALU.mult, ALU.add,
                              [f"st{blk}", "consts", f"pSt{blk}"], [f"st{blk}"])
                    S.cp(ACT, stb[par][blk][:], st[blk][:], [f"st{blk}"], [f"stb{par}_{blk}"])
                S.op(DVE, lambda e, cb2=cb2: e.bn_stats(out=bst[cb2][:], in_=pO[:]), ["pO"], [f"bst{cb2}"])
                S.op(DVE, lambda e, cb2=cb2: e.bn_aggr(out=mv[cb2][:], in_=bst[cb2][:]), [f"bst{cb2}"], [f"mv{cb2}"])
                S.act(rsd[cb2][:], mv[cb2][:, 1:2], AF.Sqrt, [f"mv{cb2}", "eps"], [f"rsd{cb2}"], bias=epsT[:], scale=1.0)
                S.op(DVE, lambda e, cb2=cb2: e.reciprocal(out=rsd[cb2][:], in_=rsd[cb2][:]), [f"rsd{cb2}"], [f"rsd{cb2}"])
                S.ts(DVE, yn[cb2][:], pO[:], mv[cb2][:, 0:1], rsd[cb2][:], ALU.subtract, ALU.mult,
                     ["pO", f"mv{cb2}", f"rsd{cb2}"], [f"yn{cb2}"])
                S.tt(POOL, yb[cb2][:], yn[cb2][:], SG[hb][:, c4, :], ALU.mult, [f"yn{cb2}", f"SG{hb}_{c4}"], [f"yb{cb2}"])

                def fin(cb2=cb2, hb=hb, c4=c4, tk=tk, t0=t0, hd=hd):
                    for j in range(4):
                        S.tr(pT[:, j, :], yb[cb2][:, j * 128:(j + 1) * 128], idb[:], [f"yb{cb2}", "idb"], ["pT"])
                    S.cp(ACT, yTs[hb][:, :, tk], pT[:, 0:4, :], ["pT"], [f"yTs{hb}_{c4}"])
                    if c4 == 3:
                        S.dma(POOL, yT_v[:, hd * 4:(hd + 1) * 4, t0:t0 + SBK], yTs[hb][:], reads=[f"yTs{hb}_{q}" for q in range(4)], writes=["yT"])
                pending.append(fin)
    for fn in pending:
        fn()
    S.emit()


def mc_consts(c):
    inv = (10000.0 ** (-np.arange(0, 256, 2, dtype=np.float32) / np.float32(256))).astype(np.float32)
    gam = 1.0 - np.exp(np.linspace(math.log(1.0 / 32), math.log(1.0 / 512), 8, dtype=np.float32))
    lg = np.log(gam.astype(np.float32))[c].astype(np.float64)
    cst = np.zeros((128, 8), np.float32)
    cst[:, 0] = inv
    cst[:, 1] = 1.0
    cst[:, 2] = np.exp(128.0 * lg)
    idx = np.arange(128, dtype=np.float64)
    j = idx[:, None]
    i = idx[None, :]
    maskT = np.where(i >= j, np.exp(np.maximum(i - j, 0.0) * lg), 0.0) / 16.0
    qd = np.exp((idx + 1.0) * lg)
    kd = np.exp((127.0 - idx) * lg) / 16.0
    tabs = np.zeros((128, 128 + 2 * SBK), np.float32)
    tabs[:, 0:128] = maskT
    tabs[:, 128:128 + SBK] = np.tile(qd, SBK // 128)[None, :]
    tabs[:, 128 + SBK:] = np.tile(kd, SBK // 128)[None, :]
    return cst, tabs


def mc_wcols(c):
    cols = list(range(c * 256, (c + 1) * 256))
    cols += list(range(2048 + c * 256, 2048 + (c + 1) * 256))
    cols += list(range(4096 + c * 512, 4096 + (c + 1) * 512))
    cols += list(range(8192 + c * 512, 8192 + (c + 1) * 512))
    return np.array(cols)


CONVW = 31
TG = 512


def emit_mb(S, nc, nseq, slen, hT, w_in, cwT, cvec, yT, wscr, eps=1e-6):
    hT_v = hT.rearrange("(kc p) t -> p kc t", p=128)
    yT_v = yT.rearrange("(cb p) t -> p cb t", p=128)
    w_v = w_in.rearrange("(kc p) c -> p kc c", p=128)
    wst = [S.sb([128, KC, 128], F32, f"wst{i}") for i in range(2)]
    wcb = [S.sb([128, KC * 128], BF16, f"wcb{i}") for i in range(2)]
    for cb in range(48):
        b = cb % 2
        S.dma(SP, wst[b][:], w_v[:, :, cb * 128:(cb + 1) * 128], writes=[f"wst{b}"])
        S.cp(ACT if cb % 2 == 0 else DVE, wcb[b][:], wst[b][:].rearrange("p a b -> p (a b)"), [f"wst{b}"], [f"wcb{b}"])
        S.dma(SP, wscr[cb], wcb[b][:], reads=[f"wcb{b}"], writes=[f"wscr{cb}"])
    cw = S.sb([128, 16, CONVW], F32, "cw")
    cv = S.sb([128, 3, 16], F32, "cv")
    S.dma(SP, cw[:], cwT, writes=["cw"])
    S.dma(SP, cv[:], cvec, writes=["cv"])
    epsT = S.sb([128, 1], F32, "epsT")
    S.op(DVE, lambda e: e.memset(epsT[:], eps), writes=["eps"])
    ones_s = S.sb([128, 128], BF16, "ones_s")
    S.op(DVE, lambda e: e.memset(ones_s[:], 1.0 / D), writes=["ones_s"])
    hTs = [S.sb([128, KC, TG], BF16, f"hTs{i}") for i in range(2)]
    wblk = [[S.sb([128, KC * 128], BF16, f"wblk{t}_{i}") for i in range(2)] for t in range(3)]
    u = [S.sb([128, 16, TG + CONVW - 1], BF16, f"u{p}") for p in range(2)]
    cT = S.sb([128, 16, TG], F32, "cT")
    sgate = S.sb([128, 16, TG], BF16, "sgate")
    sgm = S.sb([128, TG], F32, "sgm")
    acc = [S.sb([128, TG], F32, f"acc{i}") for i in range(2)]
    cbf = [S.sb([128, TG], BF16, f"cbf{i}") for i in range(2)]
    csq = [S.sb([128, TG], BF16, f"csq{i}") for i in range(2)]
    mean = S.sb([128, TG], F32, "mean")
    msq = S.sb([128, TG], F32, "msq")
    rstd = S.sb([128, TG], F32, "rstd")
    tn = [S.sb([128, TG], F32, f"tn{i}") for i in range(2)]
    sc = [S.sb([128, TG], F32, f"sc{i}") for i in range(2)]
    yTs = S.sb([128, 16, TG], BF16, "yTs")
    pacc = [S.ps([128, TG], F32, f"pacc{i}") for i in range(3)]
    pmean = S.ps([128, TG], F32, "pmean")
    pex2 = S.ps([128, TG], F32, "pex2")
    H = CONVW - 1
    gcount = 0
    wc = 0
    for sq in range(nseq):
        for gi in range(slen // TG):
            par = gcount % 2
            hb = gcount % 2
            gcount += 1
            t0 = sq * slen + gi * TG
            S.dma(SP, hTs[hb][:], hT_v[:, :, t0:t0 + TG], writes=[f"h{hb}"])
            if gi == 0:
                S.op(POOL, lambda e, par=par: e.memset(u[par][:, :, 0:H], 0.0), writes=[f"u{par}_{cb}" for cb in range(16)])
            for cb in range(16):
                wb = wc % 2
                wc += 1
                for t in range(3):
                    S.dma(SP, wblk[t][wb][:], wscr[t * 16 + cb], reads=[f"wscr{t * 16 + cb}"], writes=[f"wblk{t}_{wb}"])
                for t in range(3):
                    for kc in range(KC):
                        S.mm(pacc[t][:], wblk[t][wb][:, kc * 128:(kc + 1) * 128], hTs[hb][:, kc, :], kc == 0, kc == KC - 1,
                             [f"wblk{t}_{wb}", f"h{hb}"], [f"pacc{t}"])
                S.act(sgm[:], pacc[1][:], AF.Sigmoid, ["pacc1"], ["sgm"])
                S.tt(DVE, u[par][:, cb, H:H + TG], pacc[0][:], sgm[:], ALU.mult, ["pacc0", "sgm"], [f"u{par}_{cb}"])
                S.act(sgate[:, cb, :], pacc[2][:], AF.Silu, ["pacc2"], [f"sgate{cb}"])
                S.ts(DVE, acc[0][:], u[par][:, cb, 0:TG], cw[:, cb, 0:1], cv[:, 0, cb:cb + 1], ALU.mult, ALU.add,
                     [f"u{par}_{cb}", "cw", "cv"], ["acc0"])
                S.ts(DVE, acc[1][:], u[par][:, cb, 1:1 + TG], cw[:, cb, 1:2], None, ALU.mult, None,
                     [f"u{par}_{cb}", "cw"], ["acc1"])
                for j in range(2, CONVW):
                    a_ = j % 2
                    S.stt(acc[a_][:], u[par][:, cb, j:j + TG], cw[:, cb, j:j + 1], acc[a_][:], ALU.mult, ALU.add,
                          [f"u{par}_{cb}", "cw", f"acc{a_}"], [f"acc{a_}"])
                S.tt(DVE, cT[:, cb, :], acc[0][:], acc[1][:], ALU.add, ["acc0", "acc1"], [f"cT{cb}"])
                S.cp(POOL, u[1 - par][:, cb, 0:H], u[par][:, cb, TG:TG + H], [f"u{par}_{cb}"], [f"u{1 - par}_{cb}"])
                sb2 = cb % 2
                S.cp(ACT, cbf[sb2][:], cT[:, cb, :], [f"cT{cb}"], [f"cbf{sb2}"])
                S.act(csq[sb2][:], cT[:, cb, :], AF.Square, [f"cT{cb}"], [f"csq{sb2}"])
                S.mm(pmean[:], ones_s[:], cbf[sb2][:], cb == 0, cb == 15, ["ones_s", f"cbf{sb2}"], ["pmean"])
                S.mm(pex2[:], ones_s[:], csq[sb2][:], cb == 0, cb == 15, ["ones_s", f"csq{sb2}"], ["pex2"])
            S.cp(ACT, mean[:], pmean[:], ["pmean"], ["mean"])
            S.act(msq[:], pmean[:], AF.Square, ["pmean"], ["msq"])
            S.tt(DVE, rstd[:], pex2[:], msq[:], ALU.subtract, ["pex2", "msq"], ["rstd"])
            S.act(rstd[:], rstd[:], AF.Sqrt, ["rstd", "eps"], ["rstd"], bias=epsT[:], scale=1.0)
            S.op(DVE, lambda e: e.reciprocal(out=rstd[:], in_=rstd[:]), ["rstd"], ["rstd"])
            for cb in range(16):
                b2 = cb % 2
                S.tt(DVE, tn[b2][:], cT[:, cb, :], mean[:], ALU.subtract, [f"cT{cb}", "mean"], [f"tn{b2}"])
                S.tt(POOL, tn[b2][:], tn[b2][:], rstd[:], ALU.mult, [f"tn{b2}", "rstd"], [f"tn{b2}"])
                S.act(sc[b2][:], tn[b2][:], AF.Silu, [f"tn{b2}", "cv"], [f"sc{b2}"], scale=cv[:, 1, cb:cb + 1], bias=cv[:, 2, cb:cb + 1])
                S.tt(POOL, yTs[:, cb, :], sc[b2][:], sgate[:, cb, :], ALU.mult, [f"sc{b2}", f"sgate{cb}"], [f"yTs{cb}"])
            S.dma(POOL, yT_v[:, :, t0:t0 + TG], yTs[:], reads=[f"yTs{cb}" for cb in range(16)], writes=["yT"])
    S.emit()


def build_model(nseq, slen):
    nc = bass.Bass("TRN2", target_bir_lowering=False)
    NH = int(os.environ.get("KDBG_NHEADS", "8"))
    STG = int(os.environ.get("KDBG_STAGES", "9"))
    ntok = nseq * slen
    NT = ntok // 128
    dt = nc.dram_tensor
    x = dt("x", [ntok, D], F32, kind="ExternalInput").ap()
    pos = dt("pos", [1, ntok], I32, kind="ExternalInput").ap()
    nw = dt("nw", [5, D], F32, kind="ExternalInput").ap()
    a_w_in = dt("a_w_in", [2, D, 10240], F32, kind="ExternalInput").ap()
    a_w_out = dt("a_w_out", [2, 1024, D], F32, kind="ExternalInput").ap()
    b_w_in = dt("b_w_in", [D, 6144], F32, kind="ExternalInput").ap()
    b_cw = dt("b_cw", [128, 16, CONVW], F32, kind="ExternalInput").ap()
    b_cv = dt("b_cv", [128, 3, 16], F32, kind="ExternalInput").ap()
    b_w_out = dt("b_w_out", [D, D], F32, kind="ExternalInput").ap()
    c_w_in = dt("c_w_in", [D, 12288], F32, kind="ExternalInput").ap()
    c_w_out = dt("c_w_out", [4096, D], F32, kind="ExternalInput").ap()
    ident = dt("ident", [128, 128], F32, kind="ExternalInput").ap()
    a_cst = dt("a_cst", [128, 8], F32, kind="ExternalInput").ap()
    a_msk = dt("a_msk", [128, 256], F32, kind="ExternalInput").ap()
    a_perm = dt("a_perm", [32, 32], F32, kind="ExternalInput").ap()
    c_cst = dt("c_cst", [8, 128, 8], F32, kind="ExternalInput").ap()
    c_tabs = dt("c_tabs", [8, 128, 128 + 2 * SBK], F32, kind="ExternalInput").ap()
    out = dt("out", [ntok, D], F32, kind="ExternalOutput").ap()
    hT = dt("hT_scr", [D, ntok], BF16, kind="Internal").ap()
    yT = dt("yT_scr", [4096, ntok], BF16, kind="Internal").ap()
    xs = dt("x_scr", [ntok, D], F32, kind="Internal").ap()
    wscr = dt("w_scr", [48, 128, 2048], BF16, kind="Internal").ap()
    S = Sched(nc)
    stages = [
        lambda: emit_opn(S, nc, NT, 0, False, x, nw[0, :], ident, hT=hT),
        lambda: emit_ma(S, nc, nseq, slen, hT, a_w_in[0], pos, ident, a_cst, a_msk, a_perm, yT[0:1024, :], nheads=NH),
        lambda: emit_opn(S, nc, NT, 1024, False, x, nw[1, :], ident, yT=yT[0:1024, :], wout=a_w_out[0], hT=hT, xo=xs),
        lambda: emit_mb(S, nc, nseq, slen, hT, b_w_in, b_cw, b_cv, yT[0:2048, :], wscr),
        lambda: emit_opn(S, nc, NT, 2048, False, xs, nw[2, :], ident, yT=yT[0:2048, :], wout=b_w_out, hT=hT, xo=xs),
        lambda: emit_mc(S, nc, nseq, slen, hT, c_w_in, pos, ident, c_cst, c_tabs, yT, nheads=NH),
        lambda: emit_opn(S, nc, NT, 4096, False, xs, nw[3, :], ident, yT=yT, wout=c_w_out, hT=hT, xo=xs),
        lambda: emit_ma(S, nc, nseq, slen, hT, a_w_in[1], pos, ident, a_cst, a_msk, a_perm, yT[0:1024, :], nheads=NH),
        lambda: emit_opn(S, nc, NT, 1024, True, xs, nw[4, :], ident, yT=yT[0:1024, :], wout=a_w_out[1], out=out),
    ]
    for st_ in stages[:STG]:
        st_()
    S.finish()
    return nc, S


def host_inputs(inputs):
    f = lambda a: np.ascontiguousarray(np.asarray(a, dtype=np.float32))
    nw = np.concatenate([f(inputs["norm_w"]), f(inputs["final_norm_w"])[None, :]], axis=0)
    cw = f(inputs["b_conv_w"])[0]
    b_cw = np.ascontiguousarray(cw.T.reshape(16, 128, CONVW).transpose(1, 0, 2))
    vec = np.stack([f(inputs["b_conv_b"])[0], f(inputs["b_ln_w"])[0], f(inputs["b_ln_b"])[0]], axis=0)
    b_cv = np.ascontiguousarray(vec.reshape(3, 16, 128).transpose(2, 0, 1))
    a_cst, a_msk, a_perm = ma_consts()
    cc = [mc_consts(c) for c in range(8)]
    shared = {
        "nw": nw, "a_w_in": f(inputs["a_w_in"]), "a_w_out": f(inputs["a_w_out"]), "b_w_in": f(inputs["b_w_in"])[0],
        "b_cw": b_cw, "b_cv": b_cv, "b_w_out": f(inputs["b_w_out"])[0], "c_w_in": f(inputs["c_w_in"])[0],
        "c_w_out": f(inputs["c_w_out"])[0], "ident": np.eye(128, dtype=np.float32), "a_cst": a_cst, "a_msk": a_msk,
        "a_perm": a_perm, "c_cst": np.stack([c[0] for c in cc]), "c_tabs": np.stack([c[1] for c in cc]),
    }
    return shared


def build_part(slen, part):
    nc = bass.Bass("TRN2", target_bir_lowering=False)
    ntok = slen
    NT = ntok // 128
    dt = nc.dram_tensor
    ei = lambda n, sh, d=F32: dt(n, sh, d, kind="ExternalInput").ap()
    eo = lambda n, sh, d=F32: dt(n, sh, d, kind="ExternalOutput").ap()
    nw = ei("nw", [5, D])
    ident = ei("ident", [128, 128])
    S = Sched(nc)
    if part == 1:
        x = ei("x", [ntok, D])
        hT = eo("hT", [D, ntok], BF16)
        emit_opn(S, nc, NT, 0, False, x, nw[0, :], ident, hT=hT)
    else:
        pos = ei("pos", [1, ntok], I32)
        hT_in = ei("hT_in", [D, ntok], BF16)
        a_w_in = ei("a_w_in", [D, 10240])
        a_w_out = ei("a_w_out", [1024, D])
        a_cst = ei("a_cst", [128, 8])
        a_msk = ei("a_msk", [128, 256])
        a_perm = ei("a_perm", [32, 32])
        yT = dt("yT_scr", [4096, ntok], BF16, kind="Internal").ap()
        if part == 2:
            x = ei("x", [ntok, D])
            b_w_in = ei("b_w_in", [D, 6144])
            b_cw = ei("b_cw", [128, 16, CONVW])
            b_cv = ei("b_cv", [128, 3, 16])
            b_w_out = ei("b_w_out", [D, D])
            c_w_in = ei("c_w_in", [D, 12288])
            c_w_out = ei("c_w_out", [4096, D])
            c_cst = ei("c_cst", [8, 128, 8])
            c_tabs = ei("c_tabs", [8, 128, 128 + 2 * SBK])
            xs = eo("xs", [ntok, D])
            hT = eo("hT", [D, ntok], BF16)
            wscr = dt("w_scr", [48, 128, 2048], BF16, kind="Internal").ap()
            emit_ma(S, nc, 1, slen, hT_in, a_w_in, pos, ident, a_cst, a_msk, a_perm, yT[0:1024, :])
            emit_opn(S, nc, NT, 1024, False, x, nw[1, :], ident, yT=yT[0:1024, :], wout=a_w_out, hT=hT, xo=xs)
            emit_mb(S, nc, 1, slen, hT, b_w_in, b_cw, b_cv, yT[0:2048, :], wscr)
            emit_opn(S, nc, NT, 2048, False, xs, nw[2, :], ident, yT=yT[0:2048, :], wout=b_w_out, hT=hT, xo=xs)
            emit_mc(S, nc, 1, slen, hT, c_w_in, pos, ident, c_cst, c_tabs, yT)
            emit_opn(S, nc, NT, 4096, False, xs, nw[3, :], ident, yT=yT, wout=c_w_out, hT=hT, xo=xs)
        else:
            xs_in = ei("xs_in", [ntok, D])
            out = eo("out", [ntok, D])
            emit_ma(S, nc, 1, slen, hT_in, a_w_in, pos, ident, a_cst, a_msk, a_perm, yT[0:1024, :])
            emit_opn(S, nc, NT, 1024, True, xs_in, nw[4, :], ident, yT=yT[0:1024, :], wout=a_w_out, out=out)
    S.finish()
    return nc


def kernel(**inputs):
    x = np.asarray(inputs["x"], dtype=np.float32)
    pos = np.asarray(inputs["positions"], dtype=np.int32)
    B, SL, _ = x.shape
    sh = host_inputs(inputs)
    cores = list(range(B))
    base = {"nw": sh["nw"], "ident": sh["ident"]}
    acon = {k: sh[k] for k in ("a_cst", "a_msk", "a_perm")}
    xb = [np.ascontiguousarray(x[b]) for b in range(B)]
    pb = [np.ascontiguousarray(pos[b:b + 1]) for b in range(B)]
    r1 = run_bass_kernel_spmd(build_part(SL, 1), [dict(base, x=xb[b]) for b in range(B)], core_ids=cores).results
    w2 = dict(base, **acon, a_w_in=sh["a_w_in"][0], a_w_out=sh["a_w_out"][0], b_w_in=sh["b_w_in"], b_cw=sh["b_cw"],
              b_cv=sh["b_cv"], b_w_out=sh["b_w_out"], c_w_in=sh["c_w_in"], c_w_out=sh["c_w_out"], c_cst=sh["c_cst"],
              c_tabs=sh["c_tabs"])
    r2 = run_bass_kernel_spmd(build_part(SL, 2), [dict(w2, x=xb[b], pos=pb[b], hT_in=np.asarray(r1[b]["hT"])) for b in range(B)],
                              core_ids=cores).results
    w3 = dict(base, **acon, a_w_in=sh["a_w_in"][1], a_w_out=sh["a_w_out"][1])
    r3 = run_bass_kernel_spmd(build_part(SL, 3), [dict(w3, pos=pb[b], hT_in=np.asarray(r2[b]["hT"]), xs_in=np.asarray(r2[b]["xs"]))
                                                  for b in range(B)], core_ids=cores).results
    return np.stack([np.asarray(r3[b]["out"], dtype=np.float32) for b in range(B)], axis=0)
```
